# Optimizing a Trainium2 kernel written in Bass

```python
import jax, jax.numpy as jnp
from jax import lax
import numpy as np

D_MODEL = 2048
BATCH = 4
SEQ = 4096
DEPTH = 4

GRID_W = 64
CTX_LEN = 256
CHUNK = 128
ROWS_PER_CHUNK = CHUNK // GRID_W
MIX_W = D_MODEL
A_WIDTH = MIX_W // 2
A_HEADS = 8
A_HEAD_DIM = A_WIDTH // A_HEADS
B_WIDTH = MIX_W - A_WIDTH
B_HEADS = 4
B_DV = B_WIDTH // B_HEADS
B_DK = B_DV // 2
B_KEY_W = B_HEADS * B_DK
GATE_RANK = 16
GATE_TAU = 16.0
FFN_HIDDEN = -(-8 * D_MODEL // (3 * 256)) * 256
P_IN = 2 * A_WIDTH + 2 * B_KEY_W + 2 * B_WIDTH + 2 * GATE_RANK
EPS = 1e-6

kernel_name = 'hybrid_gmlp_gla_prefix_dit'


def rms_norm(x, g):
    xf = x.astype(jnp.float32)
    y = xf * lax.rsqrt(jnp.mean(xf * xf, axis=-1, keepdims=True) + EPS)
    return (y * g.astype(jnp.float32)).astype(x.dtype)


def layer_norm(x, g, b):
    xf = x.astype(jnp.float32)
    mu = jnp.mean(xf, axis=-1, keepdims=True)
    var = jnp.mean(jnp.square(xf - mu), axis=-1, keepdims=True)
    y = (xf - mu) * lax.rsqrt(var + EPS)
    return (y * g.astype(jnp.float32) + b.astype(jnp.float32)).astype(x.dtype)


def modulate(h, shift, scale):
    return h * (1 + scale) + shift


def split_proj(p):
    sizes = (A_WIDTH, A_WIDTH, B_KEY_W, B_KEY_W, B_WIDTH, B_WIDTH, GATE_RANK, GATE_RANK)
    out, start = [], 0
    for s in sizes:
        out.append(p[..., start:start + s])
        start += s
    return out


def spatial_gating(u, v, ln_g, ln_b, w_s, b_s, n_chunks):
    bsz, length, _ = u.shape
    u = jax.nn.gelu(u).reshape(bsz, n_chunks, CHUNK, A_HEADS, A_HEAD_DIM)
    v = jax.nn.gelu(v).reshape(bsz, length, A_HEADS, A_HEAD_DIM)
    v = layer_norm(v, ln_g.reshape(A_HEADS, A_HEAD_DIM), ln_b.reshape(A_HEADS, A_HEAD_DIM))
    v = v.reshape(bsz, n_chunks, CHUNK, A_HEADS, A_HEAD_DIM)
    mixed = jnp.einsum('hij,bnjhd->bnihd', w_s, v) + b_s.T[None, None, :, :, None]
    return (u * mixed).reshape(bsz, length, A_WIDTH)


def gla_prep(q, k, v, d_f, d_b, wd2_f, bd_f, wd2_b, bd_b):
    bsz, length, _ = q.shape
    def heads(t, d):
        return t.astype(jnp.float32).reshape(bsz, length, B_HEADS, d)
    q = heads(q, B_DK) * (B_DK ** -0.5)
    k = heads(k, B_DK)
    v = heads(v, B_DV)
    la_f = heads(jax.nn.log_sigmoid((d_f @ wd2_f + bd_f).astype(jnp.float32)) / GATE_TAU, B_DK)
    la_b = heads(jax.nn.log_sigmoid((d_b @ wd2_b + bd_b).astype(jnp.float32)) / GATE_TAU, B_DK)
    return q, k, v, la_f, la_b


def gla_scan(q, k, v, log_a, s0):
    bsz, length, nh, _ = q.shape
    n = length // CHUNK
    def to_chunks(t):
        return t.reshape(bsz, n, CHUNK, nh, t.shape[-1]).transpose(1, 0, 3, 2, 4)
    tri = jnp.tril(jnp.ones((CHUNK, CHUNK), dtype=bool))[None, None, :, :, None]
    def step(s, inp):
        qc, kc, vc, gc = inp
        b = jnp.cumsum(gc, axis=2)
        b_last = b[:, :, -1:, :]
        rel = jnp.where(tri, b[:, :, :, None, :] - b[:, :, None, :, :], -jnp.inf)
        scores = jnp.einsum('bhid,bhjd,bhijd->bhij', qc, kc, jnp.exp(rel))
        o = jnp.einsum('bhij,bhje->bhie', scores, vc) + jnp.einsum('bhid,bhde->bhie', qc * jnp.exp(b), s)
        s = jnp.exp(b_last[:, :, 0, :, None]) * s + jnp.einsum('bhjd,bhje->bhde', kc * jnp.exp(b_last - b), vc)
        return s, o
    s_fin, o = lax.scan(step, s0, (to_chunks(q), to_chunks(k), to_chunks(v), to_chunks(log_a)))
    o = o.transpose(1, 0, 3, 2, 4).reshape(bsz, length, nh, -1)
    return o, s_fin


def bidir_gla(px, pc):
    qx, kx, vx, fx, bx = px
    qc, kc, vc, fc, bc = pc
    s0 = jnp.zeros((qx.shape[0], B_HEADS, B_DK, B_DV), jnp.float32)
    flip = lambda t: jnp.flip(t, axis=1)
    oc_f, sc_f = gla_scan(qc, kc, vc, fc, s0)
    ox_f, _ = gla_scan(qx, kx, vx, fx, sc_f)
    oc_b, sc_b = gla_scan(flip(qc), flip(kc), flip(vc), flip(bc), s0)
    ox_b, _ = gla_scan(flip(qx), flip(kx), flip(vx), flip(bx), sc_b)
    return ox_f + flip(ox_b), oc_f + flip(oc_b)


def gla_out(o, g, out_g):
    bsz, length = o.shape[:2]
    y = o * lax.rsqrt(jnp.mean(o * o, axis=-1, keepdims=True) + EPS) * out_g.astype(jnp.float32).reshape(B_HEADS, B_DV)
    return y.reshape(bsz, length, B_WIDTH).astype(g.dtype) * jax.nn.silu(g)


def swiglu(h, w_in, w_out):
    a, b = jnp.split(h @ w_in, 2, axis=-1)
    return (jax.nn.silu(a) * b) @ w_out


def trunk_layer(x, xc, c_act, cc_act, w_mod, b_mod, g_pre_mix, g_post_mix, g_pre_ffn, g_post_ffn, w_in, ln_g, ln_b, w_s, b_s, wd2_f, bd_f, wd2_b, bd_b, out_g, w_out, w_ffn_in, w_ffn_out, rows, update_ctx):
    mod_x = jnp.split((c_act @ w_mod + b_mod)[:, None, :], 6, axis=-1)
    mod_c = jnp.split(cc_act @ w_mod + b_mod, 6, axis=-1)
    hx = modulate(rms_norm(x, g_pre_mix), mod_x[0], mod_x[1])
    hc = modulate(rms_norm(xc, g_pre_mix), mod_c[0], mod_c[1])
    ux, vx, qx, kx, vvx, gx, dfx, dbx = split_proj(hx @ w_in)
    uc, vc, qc, kc, vvc, gc, dfc, dbc = split_proj(hc @ w_in)
    gla_x, gla_c = bidir_gla(gla_prep(qx, kx, vvx, dfx, dbx, wd2_f, bd_f, wd2_b, bd_b),
                             gla_prep(qc, kc, vvc, dfc, dbc, wd2_f, bd_f, wd2_b, bd_b))
    a_x = spatial_gating(ux, vx, ln_g, ln_b, w_s, b_s, rows // ROWS_PER_CHUNK)
    y_x = jnp.concatenate([a_x, gla_out(gla_x, gx, out_g)], axis=-1) @ w_out
    x = x + mod_x[2] * rms_norm(y_x, g_post_mix)
    f_x = swiglu(modulate(rms_norm(x, g_pre_ffn), mod_x[3], mod_x[4]), w_ffn_in, w_ffn_out)
    x = x + mod_x[5] * rms_norm(f_x, g_post_ffn)
    if update_ctx:
        a_c = spatial_gating(uc, vc, ln_g, ln_b, w_s, b_s, xc.shape[1] // CHUNK)
        y_c = jnp.concatenate([a_c, gla_out(gla_c, gc, out_g)], axis=-1) @ w_out
        xc = xc + mod_c[2] * rms_norm(y_c, g_post_mix)
        f_c = swiglu(modulate(rms_norm(xc, g_pre_ffn), mod_c[3], mod_c[4]), w_ffn_in, w_ffn_out)
        xc = xc + mod_c[5] * rms_norm(f_c, g_post_ffn)
    return x, xc


def setup_inputs(seed: int = 0) -> dict:
    key = jax.random.key(seed)
    ks = iter(jax.random.split(key, 32))
    f32 = jnp.float32
    def nrm(shape, scale=1.0):
        return jax.random.normal(next(ks), shape, f32) * scale
    def gain(shape):
        return 1.0 + nrm(shape, 0.05)
    L = DEPTH
    return {
        'x': nrm((BATCH, SEQ, D_MODEL)),
        'c': nrm((BATCH, D_MODEL)),
        'ctx': nrm((BATCH, CTX_LEN, D_MODEL)),
        'c_ctx': nrm((D_MODEL,)),
        'w_mod': nrm((L, D_MODEL, 6 * D_MODEL), 0.5 * D_MODEL ** -0.5),
        'b_mod': nrm((L, 6 * D_MODEL), 0.01),
        'g_pre_mix': gain((L, D_MODEL)),
        'g_post_mix': gain((L, D_MODEL)),
        'g_pre_ffn': gain((L, D_MODEL)),
        'g_post_ffn': gain((L, D_MODEL)),
        'w_in': nrm((L, D_MODEL, P_IN), D_MODEL ** -0.5),
        'gmlp_ln_g': gain((L, A_WIDTH)),
        'gmlp_ln_b': nrm((L, A_WIDTH), 0.01),
        'gmlp_ws': nrm((L, A_HEADS, CHUNK, CHUNK), CHUNK ** -0.5),
        'gmlp_bs': gain((L, A_HEADS, CHUNK)),
        'gla_wd2_fwd': nrm((L, GATE_RANK, B_KEY_W), GATE_RANK ** -0.5),
        'gla_bd_fwd': nrm((L, B_KEY_W), 0.1),
        'gla_wd2_bwd': nrm((L, GATE_RANK, B_KEY_W), GATE_RANK ** -0.5),
        'gla_bd_bwd': nrm((L, B_KEY_W), 0.1),
        'gla_out_g': gain((L, B_WIDTH)),
        'w_out': nrm((L, MIX_W, D_MODEL), MIX_W ** -0.5),
        'w_ffn_in': nrm((L, D_MODEL, 2 * FFN_HIDDEN), D_MODEL ** -0.5),
        'w_ffn_out': nrm((L, FFN_HIDDEN, D_MODEL), FFN_HIDDEN ** -0.5),
    }


def reference(x, c, ctx, c_ctx, w_mod, b_mod, g_pre_mix, g_post_mix, g_pre_ffn, g_post_ffn, w_in, gmlp_ln_g, gmlp_ln_b, gmlp_ws, gmlp_bs, gla_wd2_fwd, gla_bd_fwd, gla_wd2_bwd, gla_bd_bwd, gla_out_g, w_out, w_ffn_in, w_ffn_out):
    rows = x.shape[1] // GRID_W
    c_act = jax.nn.silu(c)
    cc_act = jax.nn.silu(c_ctx)
    xc = ctx
    for i in range(DEPTH):
        x, xc = trunk_layer(x, xc, c_act, cc_act, w_mod[i], b_mod[i], g_pre_mix[i], g_post_mix[i], g_pre_ffn[i], g_post_ffn[i], w_in[i], gmlp_ln_g[i], gmlp_ln_b[i], gmlp_ws[i], gmlp_bs[i], gla_wd2_fwd[i], gla_bd_fwd[i], gla_wd2_bwd[i], gla_bd_bwd[i], gla_out_g[i], w_out[i], w_ffn_in[i], w_ffn_out[i], rows, i < DEPTH - 1)
    return x
```

```python
import numpy as np
from contextlib import ExitStack
import concourse.bass as bass
import concourse.mybir as mybir
from concourse.bass_utils import run_bass_kernel_spmd

F32 = mybir.dt.float32
BF16 = mybir.dt.bfloat16
AF = mybir.ActivationFunctionType
ALU = mybir.AluOpType

D = 2048
L = 4
NCH = 18
NTOK = NCH * 128
PIN = 5152
FH = 5632
EPS = 1e-6
RING = 4
STOP_AFTER = None
FUSE = True
RING_GLOBAL = False
NT_LEVEL = 3


class StopBuild(Exception):
    pass


class CleanCtx:
    def __init__(self, cm):
        self.cm = cm

    def __enter__(self):
        return self.cm.__enter__()

    def __exit__(self, *a):
        self.cm.__exit__(None, None, None)
        return False


def stop_if(tag):
    if STOP_AFTER == tag:
        raise StopBuild()


class Sem:
    def __init__(self, h, name):
        self.h, self.name, self.cnt = h, name, 0


class Buf:
    def __init__(self, t=None):
        self.t = t
        self.readys = []
        self.frees = []

    def ww(self):
        return list(self.frees) + list(self.readys)

    def written(self, tok, reset=True):
        if reset:
            self.readys = [tok]
            self.frees = []
        else:
            self.readys.append(tok)

    def rd(self, tok):
        self.frees.append(tok)


class SlotView:
    def __init__(self, buf):
        self.buf = buf
        self.t = buf.t

    @property
    def readys(self):
        return self.buf.readys


class KB:
    ENG = ["pe", "act", "dve", "pool", "sp"]

    def __init__(self, nc, es):
        self.nc, self.es = nc, es
        self.ops = {e: [] for e in self.ENG}
        self.prog = {e: self.sem("prog_" + e) for e in self.ENG}
        self.waited = {e: {} for e in self.ENG}
        self.dsems = []
        self.dcache = {}
        self.ccs = self.sem('ccs')

    def sem(self, name):
        return Sem(self.es.enter_context(self.nc.semaphore(name)), name)

    def dsem(self, name):
        if name in self.dcache:
            return self.dcache[name]
        s = self.sem(name)
        self.dsems.append(s)
        self.dcache[name] = s
        return s

    def _filter(self, eng, waits):
        res = {}
        for w in waits:
            if w is None:
                continue
            s, v = w
            if self.waited[eng].get(s.name, 0) >= v:
                continue
            if res.get(s.name, (None, 0))[1] < v:
                res[s.name] = (s, v)
        for s, v in res.values():
            self.waited[eng][s.name] = v
        return [(s.h, v) for s, v in res.values()]

    def op(self, eng, fn, waits=(), inc=True):
        ws = self._filter(eng, waits)
        tok = None
        if inc:
            p = self.prog[eng]
            p.cnt += 1
            tok = (p, p.cnt)
            self.ops[eng].append((ws, fn, p.h, 1))
        else:
            self.ops[eng].append((ws, fn, None, 0))
        return tok

    def dma(self, q, out, in_, ds, waits=()):
        ws = self._filter(q, waits)
        ds.cnt += 16
        self.ops[q].append((ws, lambda e: e.dma_start(out=out, in_=in_), ds.h, 16))
        return (ds, ds.cnt)

    def barrier(self):
        toks = [(self.prog[e], self.prog[e].cnt) for e in self.ENG if self.prog[e].cnt > 0]
        toks += [(s, s.cnt) for s in self.dsems if s.cnt > 0]
        for e in self.ENG:
            ws = self._filter(e, toks)
            if ws:
                self.ops[e].append((ws, None, None, 0))

    def replay(self, eng, e):
        for ws, fn, sh, amt in self.ops[eng]:
            for h, v in ws:
                e.wait_ge(h, v)
            if fn is not None:
                ins = fn(e)
                if sh is not None:
                    ins.then_inc(sh, amt)


def build_program():
    nc = bass.Bass("TRN2", target_bir_lowering=False)

    def din(name, shape):
        return nc.dram_tensor(name, list(shape), F32, kind="ExternalInput").ap()

    xin = din("xin", [NTOK, D])
    cvec = din("cvec", [2, D])
    consts = din("consts", [128, 1408])
    sel = din("sel", [128, 2])
    w_mod = din("w_mod", [L, D, 6 * D])
    b_mod = din("b_mod", [L, 6 * D])
    g_pre_mix = din("g_pre_mix", [L, D])
    g_post_mix = din("g_post_mix", [L, D])
    g_pre_ffn = din("g_pre_ffn", [L, D])
    g_post_ffn = din("g_post_ffn", [L, D])
    w_in = din("w_in", [L, D, PIN])
    w_d = din("w_d", [L, D, 32])
    ln_g = din("ln_g", [L, 1024])
    ln_b = din("ln_b", [L, 1024])
    wsT = din("wsT", [L, 128, 8, 128])
    bsr = din("bsr", [L, 1024])
    wdaug = din("wdaug", [L, 33, 1024])
    out_g = din("out_g", [L, 1024])
    w_out = din("w_out", [L, D, D])
    w_ffn_in = din("w_ffn_in", [L, D, 2 * FH])
    w_ffn_out = din("w_ffn_out", [L, FH, D])
    out = nc.dram_tensor("out", [2048, D], F32, kind="ExternalOutput").ap()

    def scr(name, shape, dt):
        return nc.dram_tensor(name, list(shape), dt).ap()

    XS = scr("XS", [NTOK, D], F32)
    MOD = scr("MOD", [L, 2, 6 * D], F32)
    ATs = scr("ATs", [8, 128, NTOK], BF16)
    GOTs = scr("GOTs", [8, 128, NTOK], BF16)
    QTs = scr("QTs", [4, 128, NTOK], BF16)
    KTs = scr("KTs", [4, 128, NTOK], BF16)
    Vs = scr("Vs", [NCH, 128, 1024], BF16)
    Gs = scr("Gs", [NCH, 128, 1024], BF16)
    OFs = scr("OFs", [NCH, 128, 1024], F32)
    ACTT = scr("ACTT", [44, 128, NTOK], BF16)
    YS = scr("YS", [NCH, 128, D], F32)
    SXI = nc.dram_tensor("SXI", [128, 1024], F32)
    SXO = nc.dram_tensor("SXO", [256, 1024], F32)

    def pbc(ap1d_row, n=128):
        return bass.AP(ap1d_row.tensor, ap1d_row.offset, [[0, n], [1, ap1d_row.shape[-1]]])

    with ExitStack() as es:
        k = KB(nc, es)

        uniq = [0]

        def sb(name, shape, dt, scope=None):
            uniq[0] += 1
            return (scope or es).enter_context(CleanCtx(nc.sbuf_tensor(f"{name}_{uniq[0]}", list(shape), dt)))

        CONST = sb("CONST", [128, 1408], F32)
        IDB = sb("IDB", [128, 128], BF16)
        SEL = sb("SEL", [128, 2], F32)
        ONESB = sb("ONESB", [1, 128], BF16)
        ident = CONST[:, 0:128]
        trif = CONST[:, 128:256]
        trir = CONST[:, 256:384]
        trif4 = CONST[:, 384:896]
        trir4 = CONST[:, 896:1408]
        ring = [Buf(None) for i in range(RING)]

        def alloc_ring(scope):
            if RING_GLOBAL:
                if ring[0].t is None:
                    for i in range(RING):
                        ring[i].t = sb(f"ring{i}", [128, 16, 512], BF16)
                return
            for i in range(RING):
                ring[i].t = sb(f"ring{i}", [128, 16, 512], BF16, scope)
        if RING_GLOBAL:
            alloc_ring(None)
        ring_ds = [k.dsem(f"ringds{i}") for i in range(RING)]
        VFM = sb("VFM", [128, 16, 14], F32)
        AM = sb("AM", [128, 4, 16], F32)
        DAUG = sb("DAUG", [33, NTOK], F32)
        WD = sb("WD", [33, 1024], F32)
        banks = [Buf(es.enter_context(nc.psum_tensor(f"bank{i}", [128, 512], F32))) for i in range(8)]
        ds_misc = k.dsem("ds_misc")

        def load(q, dst_ap, src_ap, ds, waits=()):
            return k.dma(q, dst_ap, src_ap, ds, waits)

        t_c = load("sp", CONST[:], consts, ds_misc)
        t_c = load("sp", SEL[:], sel, ds_misc)
        t_idb = k.dma("pool", IDB[:], consts[:, 0:128], ds_misc)
        CTOK = [(ds_misc, ds_misc.cnt)]
        t_ones = k.op("dve", lambda e: e.memset(ONESB[:], 1.0))
        t_dones = k.op("dve", lambda e: e.memset(DAUG[32:33, :], 1.0))
        CTOK += [t_ones, t_dones]

        class Prefetch:
            def __init__(self):
                self.tiles = []
                self.emitted = 0
                self.done_upto = []

            def add(self, fn):
                self.tiles.append(fn)
                self.done_upto.append(False)
                return len(self.tiles) - 1

            def pump(self):
                while self.emitted < len(self.tiles):
                    j = self.emitted
                    if j - RING >= 0 and not self.done_upto[j - RING]:
                        break
                    slot = ring[j % RING]
                    ds = ring_ds[j % RING]
                    waits = slot.ww()
                    toks = []

                    def emit(dst, src, waits=waits, ds=ds, toks=toks):
                        toks.append(k.dma("pool", dst, src, ds, waits))
                    self.tiles[j](slot.t, emit)
                    slot.written(toks[-1])
                    self.emitted += 1

            def get(self, i):
                self.pump()
                assert self.emitted > i, (i, self.emitted)
                return SlotView(ring[i % RING])

            def done(self, i, tok):
                ring[i % RING].rd(tok)
                self.done_upto[i] = True
                self.pump()

        pf = Prefetch()

        def wtile_cols(wl, c0, ncols=512):
            src = wl.rearrange("(kc p) n -> p kc n", p=128)[:, :, c0:c0 + ncols]

            def fn(slot, emit):
                emit(slot[:, :, 0:ncols], src)
            return fn

        bank_rr = [0]

        def next_bank():
            b = banks[bank_rr[0] % 8]
            bank_rr[0] += 1
            return b

        class Stage:
            def __init__(self, name, shape, dt, n, scope):
                self.bufs = [Buf(sb(f"{name}{i}", shape, dt, scope)) for i in range(n)]
                self.ds = [k.dsem(f"{name}ds{i}") for i in range(n)]
                self.i = 0

            def next(self):
                b, ds = self.bufs[self.i % len(self.bufs)], self.ds[self.i % len(self.bufs)]
                self.i += 1
                return b, ds

        def build_body():
            with ExitStack() as ps:
                alloc_ring(ps)
                CV = sb("CV", [2, D], F32, ps)
                CT = sb("CT", [128, 16, 2], BF16, ps)
                BM = sb("BM", [2, 6 * D], F32, ps)
                MROW = sb("MROW", [2, 6 * D], F32, ps)
                ds_p = k.dsem("ds_p")
                ds_bm = k.dsem("ds_bm")
                ds_mo = k.dsem("ds_mo")
                t = load("sp", CV[:], cvec, ds_p)
                t = k.op("act", lambda e: e.activation(out=CV[:], in_=CV[:], func=AF.Silu), waits=[t])
                b0 = banks[0]
                for fc in range(16):
                    tk = k.op("pe", lambda e, fc=fc: e.transpose(b0.t[:, fc * 2:fc * 2 + 2], CV[0:2, fc * 128:(fc + 1) * 128], ident[0:2, 0:2]),
                              waits=[t] + CTOK, inc=(fc == 15))
                b0.written(tk)
                t_ct = k.op("dve", lambda e: e.tensor_copy(out=CT[:].rearrange("p a b -> p (a b)"), in_=b0.t[:, 0:32]), waits=[tk])
                b0.rd(t_ct)
                for l in range(L):
                    tb = load("sp", BM[:], pbc(b_mod[l:l + 1, :], 2), ds_bm, waits=BMfree if l > 0 else ())
                    mrow_toks = []
                    for cb in range(24):
                        ti = pf.add(wtile_cols(w_mod[l], cb * 512))
                        slot = pf.get(ti)
                        bk = next_bank()
                        for kc in range(16):
                            tk = k.op("pe", lambda e, kc=kc, slot=slot, bk=bk: e.matmul(bk.t[0:2, :], lhsT=CT[:, kc, :], rhs=slot.t[:, kc, :], start=(kc == 0), stop=(kc == 15)),
                                      waits=slot.readys + bk.ww() + [t_ct], inc=(kc == 15))
                        bk.written(tk)
                        pf.done(ti, tk)
                        te = k.op("dve", lambda e, cb=cb, bk=bk: e.tensor_tensor(out=MROW[:, cb * 512:(cb + 1) * 512], in0=bk.t[0:2, :], in1=BM[:, cb * 512:(cb + 1) * 512], op=ALU.add),
                                  waits=[tk, tb] + (MROWfree if l > 0 else []))
                        bk.rd(te)
                        mrow_toks.append(te)
                    ts = k.dma("sp", MOD[l], MROW[:], ds_mo, waits=[mrow_toks[-1]])
                    MROWfree = [ts]
                    BMfree = [mrow_toks[-1]]
                k.barrier()
            stop_if('prologue')

            def rstd_from_ss(ss_ap, sd_ap, rs_ap, scale, t_ss):
                t1 = k.op("act", lambda e: e.activation(out=sd_ap, in_=ss_ap, func=AF.Sqrt, scale=scale, bias=EPS), waits=[t_ss])
                t2 = k.op("dve", lambda e: e.reciprocal(out=rs_ap, in_=sd_ap), waits=[t1])
                return t2

            def norm_transpose(src_rows, chunks, ai_x, ai_c, bi_x, bi_c, HT, HTb, ltoks):
                with ExitStack() as ps:
                    XT = [Buf(sb(f"nt_xt{i}", [128, D], F32, ps)) for i in range(2)]
                    XN = [Buf(sb(f"nt_xn{i}", [128, D], F32, ps)) for i in range(2)]
                    JK = Buf(sb("nt_jk", [128, D], BF16, ps))
                    ST = sb("nt_st", [128, 2, 4], F32, ps)
                    xds = [k.dsem("nt_xds0"), k.dsem("nt_xds1")]
                    STb = [Buf(), Buf()]
                    for n, c in enumerate(chunks):
                        s = n % 2
                        xt, xn, stb = XT[s], XN[s], STb[s]
                        tl = k.dma("sp", xt.t[:], src_rows[c * 128:(c + 1) * 128, :], xds[s], waits=xt.ww())
                        xt.written(tl)
                        tsq = k.op("act", lambda e, xt=xt, s=s: e.activation(out=JK.t[:], in_=xt.t[:], func=AF.Square, accum_out=ST[:, s, 0:1]),
                                   waits=[tl] + JK.ww() + stb.ww())
                        JK.written(tsq)
                        trs = rstd_from_ss(ST[:, s, 0:1], ST[:, s, 1:2], ST[:, s, 2:3], 1.0 / D, tsq)
                        tn = k.op("dve", lambda e, xt=xt, xn=xn, s=s: e.tensor_scalar(out=xn.t[:], in0=xt.t[:], scalar1=ST[:, s, 2:3], scalar2=None, op0=ALU.mult),
                                  waits=[trs, tl] + xn.ww())
                        xt.rd(tn)
                        xt.rd(tsq)
                        stb.written(tn)
                        xn.written(tn)
                        ai, bi = (ai_c, bi_c) if c < 2 else (ai_x, bi_x)
                        evs = []
                        if NT_LEVEL < 2:
                            xn.rd(tn)
                            continue
                        for g4 in range(4):
                            bk = banks[(n % 2) * 4 + g4]
                            for q in range(4):
                                fc = g4 * 4 + q
                                tk = k.op("pe", lambda e, bk=bk, q=q, fc=fc, xn=xn: e.transpose(bk.t[:, q * 128:(q + 1) * 128], xn.t[:, fc * 128:(fc + 1) * 128], ident),
                                          waits=[tn] + bk.ww() + CTOK, inc=(q == 3))
                            bk.written(tk)
                            if NT_LEVEL < 3:
                                continue
                            for q in range(4):
                                fc = g4 * 4 + q
                                eng = "act" if (g4 % 2 == 0) else "dve"
                                if eng == "act":
                                    te = k.op("act", lambda e, bk=bk, q=q, fc=fc, c=c, ai=ai, bi=bi: e.activation(
                                        out=HT[:, fc, c * 128:(c + 1) * 128], in_=bk.t[:, q * 128:(q + 1) * 128], func=AF.Identity,
                                        scale=AM[:, ai, fc:fc + 1], bias=VFM[:, fc, bi:bi + 1]), waits=[tk] + ltoks + HTb[c].ww())
                                else:
                                    te = k.op("dve", lambda e, bk=bk, q=q, fc=fc, c=c, ai=ai, bi=bi: e.tensor_scalar(
                                        out=HT[:, fc, c * 128:(c + 1) * 128], in0=bk.t[:, q * 128:(q + 1) * 128],
                                        scalar1=AM[:, ai, fc:fc + 1], scalar2=VFM[:, fc, bi:bi + 1], op0=ALU.mult, op1=ALU.add), waits=[tk] + ltoks + HTb[c].ww())
                                bk.rd(te)
                                evs.append(te)
                        xn.rd(tk)
                        HTb[c].readys = evs
                        HTb[c].frees = []
                    k.barrier()

            def tt_list(chunks):
                res = []
                if chunks[0] == 0:
                    res.append((0, 256))
                res += [(256 + 512 * i, 512) for i in range(4)]
                return res

            def tm_matmul_to_ys(chunks, groups, act_fn):
                with ExitStack() as ps:
                    stg = Stage("tm_stg", [128, 512], F32, 3, ps)
                    for (tis, kcs, col0) in groups:
                        slots = [pf.get(ti) for ti in tis]
                        nk = sum(kcs)
                        last = None
                        for c in chunks:
                            bk = next_bank()
                            kk = 0
                            for slot, kc_n in zip(slots, kcs):
                                for kq in range(kc_n):
                                    lhsT, aw = act_fn(c, kk)
                                    last = k.op("pe", lambda e, bk=bk, lhsT=lhsT, slot=slot, kq=kq, kk=kk, nk=nk: e.matmul(
                                        bk.t[:], lhsT=lhsT, rhs=slot.t[:, kq, :], start=(kk == 0), stop=(kk == nk - 1)),
                                        waits=slot.readys + aw + bk.ww(), inc=(kk == nk - 1))
                                    kk += 1
                            bk.written(last)
                            sg, ds = stg.next()
                            eng = "act" if (c % 2 == 0) else "dve"
                            if eng == "act":
                                te = k.op("act", lambda e, sg=sg, bk=bk: e.activation(out=sg.t[:], in_=bk.t[:], func=AF.Identity), waits=[last] + sg.ww())
                            else:
                                te = k.op("dve", lambda e, sg=sg, bk=bk: e.tensor_copy(out=sg.t[:], in_=bk.t[:]), waits=[last] + sg.ww())
                            bk.rd(te)
                            sg.written(te)
                            tst = k.dma("sp", YS[c][:, col0:col0 + 512], sg.t[:], ds, waits=[te])
                            sg.rd(tst)
                        for ti in tis:
                            pf.done(ti, last)
                    k.barrier()

            def epilogue_pass(l, chunks, src_rows, dst_fn, gate_idx, g_post):
                with ExitStack() as ps:
                    GX = sb("ep_gx", [128, D], F32, ps)
                    GC = sb("ep_gc", [128, D], F32, ps)
                    TMP = sb("ep_tmp", [128, D], F32, ps)
                    Y = [Buf(sb(f"ep_y{i}", [128, D], F32, ps)) for i in range(2)]
                    XT = [Buf(sb(f"ep_x{i}", [128, D], F32, ps)) for i in range(2)]
                    JK = Buf(sb("ep_jk", [128, D], BF16, ps))
                    ST = sb("ep_st", [128, 2, 4], F32, ps)
                    STb = [Buf(), Buf()]
                    yds = [k.dsem("ep_yds0"), k.dsem("ep_yds1")]
                    xds = [k.dsem("ep_xds0"), k.dsem("ep_xds1")]
                    ods = [k.dsem("ep_ods0"), k.dsem("ep_ods1")]
                    dsg = k.dsem("ep_dsg")
                    load("sp", GX[:], pbc(MOD[l, 0:1, gate_idx * D:(gate_idx + 1) * D]), dsg)
                    load("sp", GC[:], pbc(MOD[l, 1:2, gate_idx * D:(gate_idx + 1) * D]), dsg)
                    tg = load("sp", TMP[:], pbc(g_post[l:l + 1, :]), dsg)
                    tg1 = k.op("pool", lambda e: e.tensor_tensor(out=GX[:], in0=GX[:], in1=TMP[:], op=ALU.mult), waits=[tg])
                    tg2 = k.op("pool", lambda e: e.tensor_tensor(out=GC[:], in0=GC[:], in1=TMP[:], op=ALU.mult), waits=[tg, tg1])
                    def issue_loads(n):
                        c = chunks[n]
                        s = n % 2
                        y, xt = Y[s], XT[s]
                        ty = k.dma("sp", y.t[:], YS[c], yds[s], waits=y.ww())
                        y.written(ty)
                        tx = k.dma("sp", xt.t[:], src_rows[c * 128:(c + 1) * 128, :], xds[s], waits=xt.ww())
                        xt.written(tx)
                        return ty, tx
                    pend = issue_loads(0)
                    for n, c in enumerate(chunks):
                        s = n % 2
                        y, xt, stb = Y[s], XT[s], STb[s]
                        ty, tx = pend
                        if n + 1 < len(chunks):
                            pend = issue_loads(n + 1)
                        tsq = k.op("act", lambda e, y=y, s=s: e.activation(out=JK.t[:], in_=y.t[:], func=AF.Square, accum_out=ST[:, s, 0:1]),
                                   waits=[ty] + JK.ww() + stb.ww())
                        JK.written(tsq)
                        trs = rstd_from_ss(ST[:, s, 0:1], ST[:, s, 1:2], ST[:, s, 2:3], 1.0 / D, tsq)
                        GG = GC if c < 2 else GX
                        tt_ = k.op("dve", lambda e, y=y, s=s, GG=GG: e.scalar_tensor_tensor(out=y.t[:], in0=y.t[:], scalar=ST[:, s, 2:3], in1=GG[:], op0=ALU.mult, op1=ALU.mult),
                                   waits=[trs, tsq, tg2])
                        stb.written(tt_)
                        ta = k.op("pool", lambda e, y=y, xt=xt: e.tensor_tensor(out=xt.t[:], in0=xt.t[:], in1=y.t[:], op=ALU.add), waits=[tt_, tx])
                        y.rd(ta)
                        to = k.dma("sp", dst_fn(c), xt.t[:], ods[s], waits=[ta])
                        xt.rd(to)
                    k.barrier()

            def fused_pass(le, chunks, src_rows, dst_fn, gate_idx, g_post, ai_x, ai_c, bi_x, bi_c, HT, HTb, ltoks):
                with ExitStack() as ps:
                    GX = sb("fp_gx", [128, D], F32, ps)
                    GC = sb("fp_gc", [128, D], F32, ps)
                    TMP = sb("fp_tmp", [128, D], F32, ps)
                    Y = [Buf(sb(f"fp_y{i}", [128, D], F32, ps)) for i in range(2)]
                    XT = [Buf(sb(f"fp_x{i}", [128, D], F32, ps)) for i in range(2)]
                    JK = Buf(sb("fp_jk", [128, D], BF16, ps))
                    ST = sb("fp_st", [128, 2, 8], F32, ps)
                    STb = [Buf(), Buf()]
                    yds = [k.dsem("ep_yds0"), k.dsem("ep_yds1")]
                    xds = [k.dsem("ep_xds0"), k.dsem("ep_xds1")]
                    ods = [k.dsem("ep_ods0"), k.dsem("ep_ods1")]
                    dsg = k.dsem("ep_dsg")
                    load("sp", GX[:], pbc(MOD[le, 0:1, gate_idx * D:(gate_idx + 1) * D]), dsg)
                    load("sp", GC[:], pbc(MOD[le, 1:2, gate_idx * D:(gate_idx + 1) * D]), dsg)
                    tg = load("sp", TMP[:], pbc(g_post[le:le + 1, :]), dsg)
                    tg1 = k.op("pool", lambda e: e.tensor_tensor(out=GX[:], in0=GX[:], in1=TMP[:], op=ALU.mult), waits=[tg])
                    tg2 = k.op("pool", lambda e: e.tensor_tensor(out=GC[:], in0=GC[:], in1=TMP[:], op=ALU.mult), waits=[tg, tg1])

                    def issue_loads(n):
                        c = chunks[n]
                        s = n % 2
                        y, xt = Y[s], XT[s]
                        ty = k.dma("sp", y.t[:], YS[c], yds[s], waits=y.ww())
                        y.written(ty)
                        tx = k.dma("sp", xt.t[:], src_rows[c * 128:(c + 1) * 128, :], xds[s], waits=xt.ww())
                        xt.written(tx)
                        return ty, tx
                    pend = issue_loads(0)
                    for n, c in enumerate(chunks):
                        s = n % 2
                        y, xt, stb = Y[s], XT[s], STb[s]
                        ty, tx = pend
                        if n + 1 < len(chunks):
                            pend = issue_loads(n + 1)
                        tsq = k.op("act", lambda e, y=y, s=s: e.activation(out=JK.t[:], in_=y.t[:], func=AF.Square, accum_out=ST[:, s, 0:1]),
                                   waits=[ty] + JK.ww() + stb.ww())
                        JK.written(tsq)
                        trs = rstd_from_ss(ST[:, s, 0:1], ST[:, s, 1:2], ST[:, s, 2:3], 1.0 / D, tsq)
                        GG = GC if c < 2 else GX
                        tt_ = k.op("dve", lambda e, y=y, s=s, GG=GG: e.scalar_tensor_tensor(out=y.t[:], in0=y.t[:], scalar=ST[:, s, 2:3], in1=GG[:], op0=ALU.mult, op1=ALU.mult),
                                   waits=[trs, tsq, tg2])
                        ta = k.op("pool", lambda e, y=y, xt=xt: e.tensor_tensor(out=xt.t[:], in0=xt.t[:], in1=y.t[:], op=ALU.add), waits=[tt_, tx])
                        to = k.dma("sp", dst_fn(c), xt.t[:], ods[s], waits=[ta])
                        xt.rd(to)
                        tsq2 = k.op("act", lambda e, xt=xt, s=s: e.activation(out=JK.t[:], in_=xt.t[:], func=AF.Square, accum_out=ST[:, s, 4:5]),
                                    waits=[ta] + JK.ww())
                        JK.written(tsq2)
                        trs2 = rstd_from_ss(ST[:, s, 4:5], ST[:, s, 5:6], ST[:, s, 6:7], 1.0 / D, tsq2)
                        tn = k.op("dve", lambda e, xt=xt, y=y, s=s: e.tensor_scalar(out=y.t[:], in0=xt.t[:], scalar1=ST[:, s, 6:7], scalar2=None, op0=ALU.mult),
                                  waits=[trs2, ta])
                        xt.rd(tn)
                        xt.rd(tsq2)
                        stb.written(tn)
                        ai, bi = (ai_c, bi_c) if c < 2 else (ai_x, bi_x)
                        evs = []
                        tk = None
                        for g4 in range(4):
                            bk = banks[(n % 2) * 4 + g4]
                            for q in range(4):
                                fc = g4 * 4 + q
                                tk = k.op("pe", lambda e, bk=bk, q=q, fc=fc, y=y: e.transpose(bk.t[:, q * 128:(q + 1) * 128], y.t[:, fc * 128:(fc + 1) * 128], ident),
                                          waits=[tn] + bk.ww() + CTOK, inc=(q == 3))
                            bk.written(tk)
                            for q in range(4):
                                fc = g4 * 4 + q
                                if g4 % 2 == 0:
                                    te = k.op("act", lambda e, bk=bk, q=q, fc=fc, c=c, ai=ai, bi=bi: e.activation(
                                        out=HT[:, fc, c * 128:(c + 1) * 128], in_=bk.t[:, q * 128:(q + 1) * 128], func=AF.Identity,
                                        scale=AM[:, ai, fc:fc + 1], bias=VFM[:, fc, bi:bi + 1]), waits=[tk] + ltoks + HTb[c].ww())
                                else:
                                    te = k.op("dve", lambda e, bk=bk, q=q, fc=fc, c=c, ai=ai, bi=bi: e.tensor_scalar(
                                        out=HT[:, fc, c * 128:(c + 1) * 128], in0=bk.t[:, q * 128:(q + 1) * 128],
                                        scalar1=AM[:, ai, fc:fc + 1], scalar2=VFM[:, fc, bi:bi + 1], op0=ALU.mult, op1=ALU.add), waits=[tk] + ltoks + HTb[c].ww())
                                bk.rd(te)
                                evs.append(te)
                        y.rd(tk)
                        HTb[c].readys = evs
                        HTb[c].frees = []
                    k.barrier()

            for l in range(L):
                last = (l == L - 1)
                src0 = xin if l == 0 else XS
                chunks_all = list(range(NCH))
                chunks_upd = list(range(2, NCH)) if last else chunks_all

                with ExitStack() as ps:
                    VROW = sb("VROW", [14, D], F32, ps)
                    dsl = k.dsem("dsl")
                    load("sp", VROW[0:6, :], MOD[l, 0].rearrange("(a f) -> a f", f=D), dsl)
                    load("sp", VROW[6:12, :], MOD[l, 1].rearrange("(a f) -> a f", f=D), dsl)
                    load("sp", VROW[12:13, :], g_pre_mix[l:l + 1, :], dsl)
                    load("sp", VROW[13:14, :], g_pre_ffn[l:l + 1, :], dsl)
                    tv = load("sp", WD[:], wdaug[l], dsl)
                    b0 = banks[0]
                    for fc in range(16):
                        tk = k.op("pe", lambda e, fc=fc: e.transpose(b0.t[:, fc * 14:(fc + 1) * 14], VROW[0:14, fc * 128:(fc + 1) * 128], ident[0:14, 0:14]),
                                  waits=[tv] + b0.ww() + CTOK, inc=(fc == 15))
                    b0.written(tk)
                    tvf = k.op("dve", lambda e: e.tensor_copy(out=VFM[:].rearrange("p a b -> p (a b)"), in_=b0.t[:, 0:224]), waits=[tk])
                    b0.rd(tvf)
                    lt = []
                    for ai, (si, gi) in enumerate([(1, 12), (7, 12), (4, 13), (10, 13)]):
                        lt.append(k.op("dve", lambda e, ai=ai, si=si, gi=gi: e.scalar_tensor_tensor(
                            out=AM[:, ai, :], in0=VFM[:, :, si], scalar=1.0, in1=VFM[:, :, gi], op0=ALU.add, op1=ALU.mult), waits=[tvf]))
                    ltoks = [lt[-1], (dsl, dsl.cnt)]
                    k.barrier()
                stop_if(f'LC{l}')

                with ExitStack() as ls:
                    HT = sb("HT", [128, 16, NTOK], BF16, ls)
                    HTb = [Buf() for _ in range(NCH)]
                    if l == 0:
                        norm_transpose(src0, chunks_all, 0, 1, 0, 6, HT, HTb, ltoks)
                    else:
                        fused_pass(l - 1, chunks_all, XS, lambda c: XS[c * 128:(c + 1) * 128, :], 5, g_post_ffn, 0, 1, 0, 6, HT, HTb, ltoks)
                    stop_if(f'A{l}')

                    with ExitStack() as ps:
                        alloc_ring(ps)
                        LNG = sb("LNG", [128, 1024], F32, ps)
                        LNB = sb("LNB", [128, 1024], F32, ps)
                        WST = sb("WST", [128, 8, 128], BF16, ps)
                        BSR = sb("BSR", [1, 1024], BF16, ps)
                        dsb = k.dsem("dsb")
                        load("sp", LNG[:], pbc(ln_g[l:l + 1, :]), dsb)
                        load("sp", LNB[:], pbc(ln_b[l:l + 1, :]), dsb)
                        k.dma("pool", WST[:], wsT[l], dsb)
                        k.dma("pool", BSR[:], bsr[l:l + 1, :], dsb)
                        btok = [(dsb, dsb.cnt)]
                        wl = w_in[l]
                        ti_v = [pf.add(wtile_cols(wl, 1024)), pf.add(wtile_cols(wl, 1536))]
                        ti_u = [pf.add(wtile_cols(wl, 0)), pf.add(wtile_cols(wl, 512))]
                        ti_q = pf.add(wtile_cols(wl, 2048))
                        ti_k = pf.add(wtile_cols(wl, 2560))
                        ti_vv = [pf.add(wtile_cols(wl, 3072)), pf.add(wtile_cols(wl, 3584))]
                        ti_g = [pf.add(wtile_cols(wl, 4096)), pf.add(wtile_cols(wl, 4608))]
                        ti_d = pf.add(wtile_cols(w_d[l], 0, 32))

                        with ExitStack() as p2:
                            VLN = [Buf(sb(f"VLN{i}", [128, 4, 1024], BF16, p2)) for i in range(2)]
                            GV = [Buf(sb(f"GV{i}", [128, 512], F32, p2)) for i in range(2)]
                            GU = [Buf(sb(f"GU{i}", [128, 512], F32, p2)) for i in range(2)]
                            MV = sb("MV", [128, 2, 4, 8], F32, p2)
                            AG = sb("AG", [128, 2, 4, 4], F32, p2)
                            AGb = [Buf(), Buf()]
                            stg = Stage("a_stg", [128, 512], BF16, 3, p2)
                            sv = [pf.get(ti_v[0]), pf.get(ti_v[1])]
                            su = [pf.get(ti_u[0]), pf.get(ti_u[1])]
                            lastpe = None
                            gi = 0
                            for tix, (t0, tl) in enumerate(tt_list(chunks_all)):
                                vln = VLN[tix % 2]
                                nchk = tl // 128
                                vtoks = []
                                for ci in range(nchk):
                                    c = t0 // 128 + ci
                                    for half in range(2):
                                        bk = next_bank()
                                        slot = sv[half]
                                        for kc in range(16):
                                            tk = k.op("pe", lambda e, bk=bk, kc=kc, c=c, slot=slot: e.matmul(
                                                bk.t[:], lhsT=HT[:, kc, c * 128:(c + 1) * 128], rhs=slot.t[:, kc, :], start=(kc == 0), stop=(kc == 15)),
                                                waits=slot.readys + HTb[c].readys + bk.ww(), inc=(kc == 15))
                                        bk.written(tk)
                                        lastpe = tk
                                        s = gi % 2
                                        gi += 1
                                        gv, agb = GV[s], AGb[s]
                                        tg_ = k.op("act", lambda e, gv=gv, bk=bk: e.activation(out=gv.t[:], in_=bk.t[:], func=AF.Gelu_apprx_tanh), waits=[tk] + gv.ww())
                                        bk.rd(tg_)
                                        gv.written(tg_)
                                        tb_ = None
                                        for h in range(4):
                                            tb_ = k.op("dve", lambda e, gv=gv, h=h, s=s: e.bn_stats(out=MV[:, s, h, 0:6], in_=gv.t[:, h * 128:(h + 1) * 128]),
                                                       waits=[tg_] + agb.ww())
                                        for h in range(4):
                                            tb_ = k.op("dve", lambda e, h=h, s=s: e.bn_aggr(out=AG[:, s, h, 0:2], in_=MV[:, s, h, 0:6]), waits=[tb_])
                                        tsd = k.op("act", lambda e, s=s: e.activation(out=AG[:, s, :, 2], in_=AG[:, s, :, 1], func=AF.Sqrt, scale=1.0, bias=EPS), waits=[tb_])
                                        trc = k.op("dve", lambda e, s=s: e.reciprocal(out=AG[:, s, :, 3], in_=AG[:, s, :, 2]), waits=[tsd])
                                        tn_ = None
                                        for h in range(4):
                                            tn_ = k.op("dve", lambda e, gv=gv, h=h, s=s: e.tensor_scalar(
                                                out=gv.t[:, h * 128:(h + 1) * 128], in0=gv.t[:, h * 128:(h + 1) * 128],
                                                scalar1=AG[:, s, h, 0:1], scalar2=AG[:, s, h, 3:4], op0=ALU.subtract, op1=ALU.mult), waits=[trc])
                                        agb.written(tn_)
                                        tm_ = k.op("pool", lambda e, gv=gv, half=half: e.tensor_tensor(
                                            out=gv.t[:], in0=gv.t[:], in1=LNG[:, half * 512:(half + 1) * 512], op=ALU.mult), waits=[tn_] + btok)
                                        ta_ = k.op("pool", lambda e, gv=gv, half=half, vln=vln, ci=ci: e.tensor_tensor(
                                            out=vln.t[:, ci, half * 512:(half + 1) * 512], in0=gv.t[:], in1=LNB[:, half * 512:(half + 1) * 512], op=ALU.add),
                                            waits=[tm_] + vln.frees)
                                        gv.rd(ta_)
                                        vtoks.append(ta_)
                                vln.readys = vtoks
                                vln.frees = []
                                for h in range(8):
                                    slot = su[h // 4]
                                    j = h % 4
                                    bu = next_bank()
                                    bm = next_bank()
                                    for kc in range(16):
                                        tk = k.op("pe", lambda e, bu=bu, kc=kc, slot=slot, j=j, t0=t0, tl=tl: e.matmul(
                                            bu.t[:, 0:tl], lhsT=slot.t[:, kc, j * 128:(j + 1) * 128], rhs=HT[:, kc, t0:t0 + tl], start=(kc == 0), stop=(kc == 15)),
                                            waits=slot.readys + sum([HTb[t0 // 128 + ci].readys for ci in range(nchk)], []) + bu.ww(), inc=(kc == 15))
                                    bu.written(tk)
                                    for ci in range(nchk):
                                        k.op("pe", lambda e, bm=bm, vln=vln, ci=ci, h=h: e.matmul(
                                            bm.t[:, ci * 128:(ci + 1) * 128], lhsT=vln.t[:, ci, h * 128:(h + 1) * 128], rhs=WST[:, h, :], start=True, stop=False),
                                            waits=vln.readys + bm.ww() + btok, inc=False)
                                        tm2 = k.op("pe", lambda e, bm=bm, ci=ci, h=h: e.matmul(
                                            bm.t[:, ci * 128:(ci + 1) * 128], lhsT=ONESB[0:1, :], rhs=BSR[0:1, h * 128:(h + 1) * 128], start=False, stop=True),
                                            waits=CTOK, inc=(ci == nchk - 1))
                                    bm.written(tm2)
                                    lastpe = tm2
                                    gu = GU[h % 2]
                                    tgu = k.op("act", lambda e, gu=gu, bu=bu, tl=tl: e.activation(out=gu.t[:, 0:tl], in_=bu.t[:, 0:tl], func=AF.Gelu_apprx_tanh), waits=[tk] + gu.ww())
                                    bu.rd(tgu)
                                    gu.written(tgu)
                                    sg, ds = stg.next()
                                    tmu = k.op("dve", lambda e, sg=sg, bm=bm, gu=gu, tl=tl: e.tensor_tensor(out=sg.t[:, 0:tl], in0=bm.t[:, 0:tl], in1=gu.t[:, 0:tl], op=ALU.mult),
                                               waits=[tm2, tgu] + sg.ww())
                                    bm.rd(tmu)
                                    gu.rd(tmu)
                                    sg.written(tmu)
                                    tst = k.dma("sp", ATs[h][:, t0:t0 + tl], sg.t[:, 0:tl], ds, waits=[tmu])
                                    sg.rd(tst)
                                vln.rd(lastpe)
                            for ti in ti_v + ti_u:
                                pf.done(ti, lastpe)
                            k.barrier()
                        stop_if(f'Bvu{l}')

                        with ExitStack() as p2:
                            stg = Stage("qk_stg", [128, 512], BF16, 3, p2)
                            for which, ti in ((0, ti_q), (1, ti_k)):
                                slot = pf.get(ti)
                                dstT = QTs if which == 0 else KTs
                                for j in range(4):
                                    for (t0, tl) in tt_list(chunks_all):
                                        bk = next_bank()
                                        hw = sum([HTb[t0 // 128 + ci].readys for ci in range(tl // 128)], [])
                                        for kc in range(16):
                                            tk = k.op("pe", lambda e, bk=bk, kc=kc, slot=slot, j=j, t0=t0, tl=tl: e.matmul(
                                                bk.t[:, 0:tl], lhsT=slot.t[:, kc, j * 128:(j + 1) * 128], rhs=HT[:, kc, t0:t0 + tl], start=(kc == 0), stop=(kc == 15)),
                                                waits=slot.readys + hw + bk.ww(), inc=(kc == 15))
                                        bk.written(tk)
                                        sg, ds = stg.next()
                                        if which == 0:
                                            te = k.op("act", lambda e, sg=sg, bk=bk, tl=tl: e.activation(out=sg.t[:, 0:tl], in_=bk.t[:, 0:tl], func=AF.Identity, scale=128.0 ** -0.5), waits=[tk] + sg.ww())
                                        else:
                                            te = k.op("dve", lambda e, sg=sg, bk=bk, tl=tl: e.tensor_copy(out=sg.t[:, 0:tl], in_=bk.t[:, 0:tl]), waits=[tk] + sg.ww())
                                        bk.rd(te)
                                        sg.written(te)
                                        tst = k.dma("sp", dstT[j][:, t0:t0 + tl], sg.t[:, 0:tl], ds, waits=[te])
                                        sg.rd(tst)
                                pf.done(ti, tk)
                            for which, tis in ((0, ti_vv), (1, ti_g)):
                                dstD = Vs if which == 0 else Gs
                                for half, ti in enumerate(tis):
                                    slot = pf.get(ti)
                                    for c in chunks_all:
                                        bk = next_bank()
                                        for kc in range(16):
                                            tk = k.op("pe", lambda e, bk=bk, kc=kc, c=c, slot=slot: e.matmul(
                                                bk.t[:], lhsT=HT[:, kc, c * 128:(c + 1) * 128], rhs=slot.t[:, kc, :], start=(kc == 0), stop=(kc == 15)),
                                                waits=slot.readys + HTb[c].readys + bk.ww(), inc=(kc == 15))
                                        bk.written(tk)
                                        sg, ds = stg.next()
                                        if which == 1:
                                            te = k.op("act", lambda e, sg=sg, bk=bk: e.activation(out=sg.t[:], in_=bk.t[:], func=AF.Silu), waits=[tk] + sg.ww())
                                        else:
                                            te = k.op("dve", lambda e, sg=sg, bk=bk: e.tensor_copy(out=sg.t[:], in_=bk.t[:]), waits=[tk] + sg.ww())
                                        bk.rd(te)
                                        sg.written(te)
                                        tst = k.dma("sp", dstD[c][:, half * 512:(half + 1) * 512], sg.t[:], ds, waits=[te])
                                        sg.rd(tst)
                                    pf.done(ti, tk)
                            slot = pf.get(ti_d)
                            for (t0, tl) in tt_list(chunks_all):
                                bk = next_bank()
                                hw = sum([HTb[t0 // 128 + ci].readys for ci in range(tl // 128)], [])
                                for kc in range(16):
                                    tk = k.op("pe", lambda e, bk=bk, kc=kc, slot=slot, t0=t0, tl=tl: e.matmul(
                                        bk.t[0:32, 0:tl], lhsT=slot.t[:, kc, 0:32], rhs=HT[:, kc, t0:t0 + tl], start=(kc == 0), stop=(kc == 15)),
                                        waits=slot.readys + hw + bk.ww(), inc=(kc == 15))
                                bk.written(tk)
                                te = k.op("dve", lambda e, bk=bk, t0=t0, tl=tl: e.tensor_copy(out=DAUG[0:32, t0:t0 + tl], in_=bk.t[0:32, 0:tl]), waits=[tk])
                                bk.rd(te)
                            pf.done(ti_d, tk)
                            k.barrier()
                stop_if(f'B{l}')

                with ExitStack() as ps:
                    QT = sb("QT", [128, 4, NTOK], BF16, ps)
                    KT = sb("KT", [128, 4, NTOK], BF16, ps)
                    OGB = sb("OGB", [128, 1024], F32, ps)
                    S = sb("S", [128, 4, 256], F32, ps)
                    SB = [Buf(sb(f"SBf{i}", [128, 4, 256], BF16, ps)) for i in range(2)]
                    T1 = sb("T1", [128, 256], F32, ps)
                    G0 = sb("G0", [128, 1024], F32, ps)
                    G1 = sb("G1", [128, 1024], F32, ps)
                    E1 = sb("E1", [128, 512], F32, ps)
                    SP = sb("SPt", [128, 512], F32, ps)
                    EP = sb("EP", [128, 512], F32, ps)
                    EM = sb("EM", [128, 512], F32, ps)
                    QTL = sb("QTL", [128, 4, 128], BF16, ps)
                    KTL = sb("KTL", [128, 4, 128], BF16, ps)
                    PM = sb("PM", [128, 4, 128], BF16, ps)
                    KTM = sb("KTM", [128, 4, 128], BF16, ps)
                    VC = [Buf(sb(f"VC{i}", [128, 1024], BF16, ps)) for i in range(2)]
                    GC_ = [Buf(sb(f"GCc{i}", [128, 1024], BF16, ps)) for i in range(2)]
                    OFL = [Buf(sb(f"OFL{i}", [128, 1024], F32, ps)) for i in range(2)]
                    OT = Buf(sb("OT", [128, 1024], F32, ps))
                    GOB = sb("GOB", [128, 1024], BF16, ps)
                    GST = [Buf(sb(f"GST{i}", [128, 8, 512], BF16, ps)) for i in range(2)]
                    gst_ds = [k.dsem(f"gstds{i}") for i in range(2)]
                    SS = sb("SSg", [128, 12], F32, ps)
                    JK2 = sb("JK2", [128, 256], BF16, ps)
                    dsc = k.dsem("dsc")
                    vds = [k.dsem(f"vds{i}") for i in range(2)]
                    gds = [k.dsem(f"gds{i}") for i in range(2)]
                    ofds = [k.dsem(f"ofds{i}") for i in range(2)]
                    ofsds = [k.dsem(f"ofsds{i}") for i in range(2)]
                    xds_ = k.dsem("xchg")
                    ccs = k.ccs
                    load("sp", QT[:], QTs.rearrange("h d t -> d h t"), dsc)
                    load("sp", KT[:], KTs.rearrange("h d t -> d h t"), dsc)
                    load("sp", OGB[:], pbc(out_g[l:l + 1, :]), dsc)
                    ctok = [(dsc, dsc.cnt)]
                    Zb, BTb, SCb, TRb, O0, O1, KV0, KV1 = banks
                    TRv = TRb.t[:].bitcast(BF16)

                    state = {"s_tok": None, "sb_i": 0, "vi": 0, "gsti": 0, "ogi": 0}
                    tz = k.op("pool", lambda e: e.memset(S[:].rearrange("p a b -> p (a b)"), 0.0))
                    tzb = k.op("pool", lambda e: e.memset(SB[0].t[:].rearrange("p a b -> p (a b)"), 0.0))
                    state["s_tok"] = tz
                    SB[0].written(tzb)
                    S_b = Buf()
                    S_b.written(tz)
                    wk = {n: Buf() for n in ["E1", "SP", "EP", "EM", "QTL", "KTL", "PM", "KTM", "GOB", "SS", "T1"]}

                    pre = {}

                    def issue_v(c):
                        vi = state["vi"]
                        state["vi"] += 1
                        vc = VC[vi % 2]
                        tv_ = k.dma("sp", vc.t[:], Vs[c], vds[vi % 2], waits=vc.ww())
                        vc.written(tv_)
                        return vc, tv_, vi

                    def issue_og(c):
                        oi = state["ogi"]
                        state["ogi"] += 1
                        ofl = OFL[oi % 2]
                        gcb = GC_[oi % 2]
                        tlo = k.dma("sp", ofl.t[:], OFs[c], ofds[oi % 2], waits=ofl.ww() + [(ofsds[0], ofsds[0].cnt), (ofsds[1], ofsds[1].cnt)])
                        ofl.written(tlo)
                        tlg = k.dma("sp", gcb.t[:], Gs[c], gds[oi % 2], waits=gcb.ww())
                        gcb.written(tlg)
                        return ofl, gcb, tlo, tlg

                    def gla_chunk(c, dr, need_o, second_pass, nxt=None):
                        tri = trif if dr == 0 else trir
                        tri4 = trif4 if dr == 0 else trir4
                        lastcol = 127 if dr == 0 else 0
                        if ("v", c) in pre:
                            vc, tv_, vi = pre.pop(("v", c))
                        else:
                            vc, tv_, vi = issue_v(c)
                        if second_pass:
                            if ("og", c) in pre:
                                ofl, gcb, tlo, tlg = pre.pop(("og", c))
                            else:
                                ofl, gcb, tlo, tlg = issue_og(c)
                        if nxt is not None:
                            pre[("v", nxt)] = issue_v(nxt)
                            if second_pass:
                                pre[("og", nxt)] = issue_og(nxt)
                        tz_ = k.op("pe", lambda e: e.matmul(Zb.t[:], lhsT=DAUG[0:33, c * 128:(c + 1) * 128], rhs=WD[0:33, dr * 512:(dr + 1) * 512], start=True, stop=True),
                                   waits=Zb.ww() + ltoks + CTOK)
                        Zb.written(tz_)
                        te1 = k.op("act", lambda e: e.activation(out=E1[:], in_=Zb.t[:], func=AF.Exp, scale=-1.0), waits=[tz_] + wk["E1"].ww())
                        Zb.rd(te1)
                        wk["E1"].written(te1)
                        tsp = k.op("act", lambda e: e.activation(out=SP[:], in_=E1[:], func=AF.Ln, bias=1.0), waits=[te1] + wk["SP"].ww())
                        wk["E1"].rd(tsp)
                        wk["SP"].written(tsp)
                        for h in range(4):
                            tbt = k.op("pe", lambda e, h=h: e.matmul(BTb.t[:, h * 128:(h + 1) * 128], lhsT=SP[:, h * 128:(h + 1) * 128], rhs=tri, start=True, stop=True),
                                       waits=[tsp] + BTb.ww(), inc=(h == 3))
                        BTb.written(tbt)
                        wk["SP"].rd(tbt)
                        tep = k.op("act", lambda e: e.activation(out=EP[:], in_=BTb.t[:], func=AF.Exp, scale=-1.0 / 16), waits=[tbt] + wk["EP"].ww())
                        wk["EP"].written(tep)
                        tem = k.op("act", lambda e: e.activation(out=EM[:], in_=BTb.t[:], func=AF.Exp, scale=1.0 / 16), waits=[tbt] + wk["EM"].ww())
                        wk["EM"].written(tem)
                        BTb.rd(tep)
                        BTb.rd(tem)
                        tkt = k.op("pool", lambda e: e.tensor_tensor(out=KTL[:], in0=KT[:, :, c * 128:(c + 1) * 128], in1=EM[:].rearrange("p (h i) -> p h i", i=128), op=ALU.mult),
                                   waits=[tem] + ctok + wk["KTL"].ww())
                        wk["EM"].rd(tkt)
                        wk["KTL"].written(tkt)
                        if need_o:
                            tqt = k.op("dve", lambda e: e.tensor_tensor(out=QTL[:], in0=QT[:, :, c * 128:(c + 1) * 128], in1=EP[:].rearrange("p (h i) -> p h i", i=128), op=ALU.mult),
                                       waits=[tep] + ctok + wk["QTL"].ww())
                            wk["QTL"].written(tqt)
                            for h in range(4):
                                tsc = k.op("pe", lambda e, h=h: e.matmul(SCb.t[:, h * 128:(h + 1) * 128], lhsT=KTL[:, h, :], rhs=QTL[:, h, :], start=True, stop=True),
                                           waits=[tkt, tqt] + SCb.ww(), inc=(h == 3))
                            SCb.written(tsc)
                            wk["QTL"].rd(tsc)
                            wk["KTL"].rd(tsc)
                            tpm = k.op("dve", lambda e: e.tensor_tensor(out=PM[:].rearrange("p h i -> p (h i)"), in0=SCb.t[:], in1=tri4, op=ALU.mult),
                                       waits=[tsc] + wk["PM"].ww() + CTOK)
                            SCb.rd(tpm)
                            wk["PM"].written(tpm)
                        for h in range(4):
                            ttr = k.op("pe", lambda e, h=h: e.transpose(TRv[:, h * 128:(h + 1) * 128], KTL[:, h, :], IDB[:]),
                                       waits=[tkt] + TRb.ww() + CTOK, inc=(h == 3))
                        TRb.written(ttr)
                        wk["KTL"].rd(ttr)
                        tkm = k.op("act", lambda e: e.activation(out=KTM[:].rearrange("p h i -> p (h i)"), in_=TRv[:, 0:512], func=AF.Identity), waits=[ttr] + wk["KTM"].ww())
                        TRb.rd(tkm)
                        wk["KTM"].written(tkm)
                        sbuf_cur = SB[state["sb_i"] % 2]
                        to_ = None
                        if need_o:
                            for h in range(4):
                                ob = O0 if h < 2 else O1
                                oc = (h % 2) * 256
                                k.op("pe", lambda e, h=h, ob=ob, oc=oc: e.matmul(ob.t[:, oc:oc + 256], lhsT=PM[:, h, :], rhs=vc.t[:, h * 256:(h + 1) * 256], start=True, stop=False),
                                     waits=[tpm, tv_] + ob.ww(), inc=False)
                                to_ = k.op("pe", lambda e, h=h, ob=ob, oc=oc, sbuf_cur=sbuf_cur: e.matmul(ob.t[:, oc:oc + 256], lhsT=QTL[:, h, :], rhs=sbuf_cur.t[:, h, :], start=False, stop=True),
                                           waits=[tqt] + sbuf_cur.readys, inc=(h == 3))
                            O0.written(to_)
                            O1.written(to_)
                            wk["PM"].rd(to_)
                            wk["QTL"].rd(to_)
                            sbuf_cur.rd(to_)
                        for h in range(4):
                            kb = KV0 if h < 2 else KV1
                            oc = (h % 2) * 256
                            tkv = k.op("pe", lambda e, h=h, kb=kb, oc=oc: e.matmul(kb.t[:, oc:oc + 256], lhsT=KTM[:, h, :], rhs=vc.t[:, h * 256:(h + 1) * 256], start=True, stop=True),
                                       waits=[tkm, tv_] + kb.ww(), inc=(h == 3))
                        KV0.written(tkv)
                        KV1.written(tkv)
                        wk["KTM"].rd(tkv)
                        vc.rd(tkv)
                        if to_ is not None:
                            vc.rd(to_)
                        ts_ = None
                        for h in range(4):
                            kb = KV0 if h < 2 else KV1
                            oc = (h % 2) * 256
                            ebs = EP[:, h * 128 + lastcol:h * 128 + lastcol + 1]
                            t1_ = k.op("dve", lambda e, h=h, ebs=ebs: e.tensor_scalar(out=T1[:], in0=S[:, h, :], scalar1=ebs, scalar2=None, op0=ALU.mult),
                                       waits=[tep] + S_b.ww() + wk["T1"].ww())
                            wk["T1"].written(t1_)
                            ts_ = k.op("dve", lambda e, h=h, kb=kb, oc=oc, ebs=ebs: e.scalar_tensor_tensor(out=S[:, h, :], in0=kb.t[:, oc:oc + 256], scalar=ebs, in1=T1[:], op0=ALU.mult, op1=ALU.add),
                                       waits=[tkv, t1_])
                            wk["T1"].rd(ts_)
                        KV0.rd(ts_)
                        KV1.rd(ts_)
                        wk["EP"].rd(ts_)
                        if need_o:
                            wk["EP"].rd(tqt)
                        S_b.written(ts_)
                        state["sb_i"] += 1
                        sbn = SB[state["sb_i"] % 2]
                        tsb = k.op("act", lambda e, sbn=sbn: e.activation(out=sbn.t[:].rearrange("p a b -> p (a b)"), in_=S[:].rearrange("p a b -> p (a b)"), func=AF.Identity),
                                   waits=[ts_] + sbn.ww())
                        sbn.written(tsb)
                        S_b.rd(tsb)
                        if not need_o:
                            return
                        if not second_pass:
                            ofl = OFL[vi % 2]
                            tcp0 = k.op("act", lambda e, ofl=ofl: e.activation(out=ofl.t[:, 0:512], in_=O0.t[:], func=AF.Identity), waits=[to_] + ofl.ww())
                            tcp1 = k.op("dve", lambda e, ofl=ofl: e.tensor_copy(out=ofl.t[:, 512:1024], in_=O1.t[:]), waits=[to_] + ofl.ww())
                            O0.rd(tcp0)
                            O1.rd(tcp1)
                            ofl.written(tcp0)
                            ofl.written(tcp1, reset=False)
                            tst = k.dma("sp", OFs[c], ofl.t[:], ofsds[vi % 2], waits=[tcp0, tcp1])
                            ofl.rd(tst)
                            return
                        ta0 = k.op("dve", lambda e, ofl=ofl: e.tensor_tensor(out=OT.t[:, 0:512], in0=O0.t[:], in1=ofl.t[:, 0:512], op=ALU.add), waits=[to_, tlo] + OT.ww())
                        ta1 = k.op("dve", lambda e, ofl=ofl: e.tensor_tensor(out=OT.t[:, 512:1024], in0=O1.t[:], in1=ofl.t[:, 512:1024], op=ALU.add), waits=[to_, tlo])
                        O0.rd(ta0)
                        O1.rd(ta1)
                        ofl.rd(ta1)
                        OT.written(ta1)
                        tq_ = None
                        for h in range(4):
                            tq_ = k.op("act", lambda e, h=h: e.activation(out=JK2[:], in_=OT.t[:, h * 256:(h + 1) * 256], func=AF.Square, accum_out=SS[:, h:h + 1]),
                                       waits=[ta1, ta0] + wk["SS"].ww())
                        tsd = k.op("act", lambda e: e.activation(out=SS[:, 4:8], in_=SS[:, 0:4], func=AF.Sqrt, scale=1.0 / 256, bias=EPS), waits=[tq_])
                        trc = k.op("dve", lambda e: e.reciprocal(out=SS[:, 8:12], in_=SS[:, 4:8]), waits=[tsd])
                        ty_ = None
                        for h in range(4):
                            ty_ = k.op("dve", lambda e, h=h: e.scalar_tensor_tensor(out=OT.t[:, h * 256:(h + 1) * 256], in0=OT.t[:, h * 256:(h + 1) * 256],
                                                                                   scalar=SS[:, 8 + h:9 + h], in1=OGB[:, h * 256:(h + 1) * 256], op0=ALU.mult, op1=ALU.mult),
                                       waits=[trc] + ctok)
                        wk["SS"].written(ty_)
                        tgo = k.op("pool", lambda e, gcb=gcb: e.tensor_tensor(out=GOB[:], in0=OT.t[:], in1=gcb.t[:], op=ALU.mult), waits=[ty_, tlg] + wk["GOB"].ww())
                        OT.rd(tgo)
                        gcb.rd(tgo)
                        wk["GOB"].written(tgo)
                        for kc in range(8):
                            ttr2 = k.op("pe", lambda e, kc=kc: e.transpose(TRv[:, kc * 128:(kc + 1) * 128], GOB[:, kc * 128:(kc + 1) * 128], IDB[:]),
                                        waits=[tgo] + TRb.ww(), inc=(kc == 7))
                        TRb.written(ttr2)
                        wk["GOB"].rd(ttr2)
                        if c < 2:
                            t0, tl = 0, 256
                        else:
                            t0, tl = 256 + 512 * ((c - 2) // 4), 512
                        off = c * 128 - t0
                        first_in_tile = (c == (t0 + tl) // 128 - 1)
                        last_in_tile = (c == t0 // 128)
                        if first_in_tile:
                            state["gsti"] += 1
                        gst = GST[state["gsti"] % 2]
                        tcp = k.op("act", lambda e, gst=gst, off=off: e.activation(out=gst.t[:, :, off:off + 128], in_=TRv[:].rearrange("p (a b) -> p a b", b=128), func=AF.Identity),
                                   waits=[ttr2] + gst.frees)
                        TRb.rd(tcp)
                        gst.written(tcp, reset=False)
                        if last_in_tile:
                            tst = k.dma("sp", GOTs.rearrange("a p t -> p a t")[:, :, t0:t0 + tl], gst.t[:, :, 0:tl], gst_ds[state["gsti"] % 2], waits=gst.readys)
                            gst.readys = []
                            gst.frees = [tst]

                    for c in range(NCH):
                        gla_chunk(c, 0, need_o=(not (last and c < 2)), second_pass=False, nxt=(c + 1 if c + 1 < NCH else None))
                    tso = k.dma("pool", SXI.ap(), S[:].rearrange("p a b -> p (a b)"), xds_, waits=S_b.readys)
                    S_b.rd(tso)
                    tcc = k.op("pool", lambda e: e.collective_compute("AllGather", ALU.bypass, replica_groups=[[0, 1], [2, 3], [4, 5], [6, 7]],
                                                                      ins=[SXI.ap().opt()], outs=[SXO.ap().opt()]), waits=[tso], inc=False)
                    k.ops["pool"][-1] = (k.ops["pool"][-1][0], k.ops["pool"][-1][1], ccs.h, 1)
                    ccs.cnt += 1
                    tccd = (ccs, ccs.cnt)
                    k.dma("pool", G0[:], SXO.ap()[0:128, :], xds_, waits=[tccd])
                    tg01 = k.dma("pool", G1[:], SXO.ap()[128:256, :], xds_, waits=[tccd])
                    if not last:
                        tz = k.op("pool", lambda e: e.memset(S[:].rearrange("p a b -> p (a b)"), 0.0), waits=S_b.ww())
                        S_b.written(tz)
                        state["sb_i"] += 1
                        sbn = SB[state["sb_i"] % 2]
                        tzb = k.op("pool", lambda e, sbn=sbn: e.memset(sbn.t[:].rearrange("p a b -> p (a b)"), 0.0), waits=sbn.ww())
                        sbn.written(tzb)
                        gla_chunk(1, 1, need_o=True, second_pass=True, nxt=0)
                        gla_chunk(0, 1, need_o=True, second_pass=True, nxt=None)
                    ta_ = k.op("dve", lambda e: e.tensor_scalar(out=G0[:], in0=G0[:], scalar1=SEL[:, 0:1], scalar2=None, op0=ALU.mult), waits=[tg01] + CTOK)
                    tb_ = k.op("dve", lambda e: e.scalar_tensor_tensor(out=S[:].rearrange("p a b -> p (a b)"), in0=G1[:], scalar=SEL[:, 1:2], in1=G0[:], op0=ALU.mult, op1=ALU.add),
                               waits=[ta_] + S_b.ww())
                    S_b.written(tb_)
                    state["sb_i"] += 1
                    sbn = SB[state["sb_i"] % 2]
                    tsb = k.op("act", lambda e, sbn=sbn: e.activation(out=sbn.t[:].rearrange("p a b -> p (a b)"), in_=S[:].rearrange("p a b -> p (a b)"), func=AF.Identity),
                               waits=[tb_] + sbn.ww())
                    sbn.written(tsb)
                    S_b.rd(tsb)
                    for c in range(NCH - 1, 1, -1):
                        gla_chunk(c, 1, need_o=True, second_pass=True, nxt=(c - 1 if c - 1 >= 2 else None))
                    k.barrier()

                stop_if(f'C{l}')
                with ExitStack() as ps:
                    AOT = sb("AOT", [128, 16, NTOK], BF16, ps)
                    alloc_ring(ps)
                    dsd = k.dsem("dsd")
                    load("sp", AOT[:, 0:8, :], ATs.rearrange("a p t -> p a t"), dsd)
                    load("sp", AOT[:, 8:16, :], GOTs.rearrange("a p t -> p a t"), dsd)
                    atok = [(dsd, dsd.cnt)]
                    tis = [pf.add(wtile_cols(w_out[l], n * 512)) for n in range(4)]
                    groups = [([tis[n]], [16], n * 512) for n in range(4)]
                    tm_matmul_to_ys(chunks_upd, groups, lambda c, kk: (AOT[:, kk, c * 128:(c + 1) * 128], atok))

                stop_if(f'D{l}')
                with ExitStack() as ls:
                    HT = sb("HT2", [128, 16, NTOK], BF16, ls)
                    HTb = [Buf() for _ in range(NCH)]
                    if FUSE:
                        fused_pass(l, chunks_upd, src0, lambda c: XS[c * 128:(c + 1) * 128, :], 2, g_post_mix, 2, 3, 3, 9, HT, HTb, ltoks)
                    else:
                        epilogue_pass(l, chunks_upd, src0, lambda c: XS[c * 128:(c + 1) * 128, :], 2, g_post_mix)
                        norm_transpose(XS, chunks_upd, 2, 3, 3, 9, HT, HTb, ltoks)
                    with ExitStack() as p2:
                        alloc_ring(p2)
                        SA = [Buf(sb(f"SA{i}", [128, 512], F32, p2)) for i in range(2)]
                        stg = Stage("e_stg", [128, 512], BF16, 3, p2)
                        wl = w_ffn_in[l]
                        order = []
                        for t in range(11):
                            order.append((pf.add(wtile_cols(wl, t * 512)), pf.add(wtile_cols(wl, FH + t * 512))))
                        si = 0
                        for t, (ta_i, tb_i) in enumerate(order):
                            sa_slot = pf.get(ta_i)
                            sb_slot = pf.get(tb_i)
                            for j in range(4):
                                hc = t * 4 + j
                                for (t0, tl) in tt_list(chunks_upd):
                                    ba = next_bank()
                                    bb = next_bank()
                                    hw = sum([HTb[t0 // 128 + ci].readys for ci in range(tl // 128)], [])
                                    for kc in range(16):
                                        tka = k.op("pe", lambda e, ba=ba, kc=kc, sa_slot=sa_slot, j=j, t0=t0, tl=tl: e.matmul(
                                            ba.t[:, 0:tl], lhsT=sa_slot.t[:, kc, j * 128:(j + 1) * 128], rhs=HT[:, kc, t0:t0 + tl], start=(kc == 0), stop=(kc == 15)),
                                            waits=sa_slot.readys + hw + ba.ww(), inc=(kc == 15))
                                    ba.written(tka)
                                    for kc in range(16):
                                        tkb = k.op("pe", lambda e, bb=bb, kc=kc, sb_slot=sb_slot, j=j, t0=t0, tl=tl: e.matmul(
                                            bb.t[:, 0:tl], lhsT=sb_slot.t[:, kc, j * 128:(j + 1) * 128], rhs=HT[:, kc, t0:t0 + tl], start=(kc == 0), stop=(kc == 15)),
                                            waits=sb_slot.readys + hw + bb.ww(), inc=(kc == 15))
                                    bb.written(tkb)
                                    sa = SA[si % 2]
                                    si += 1
                                    tsl = k.op("act", lambda e, sa=sa, ba=ba, tl=tl: e.activation(out=sa.t[:, 0:tl], in_=ba.t[:, 0:tl], func=AF.Silu), waits=[tka] + sa.ww())
                                    ba.rd(tsl)
                                    sa.written(tsl)
                                    sg, ds = stg.next()
                                    tmu = k.op("dve", lambda e, sg=sg, bb=bb, sa=sa, tl=tl: e.tensor_tensor(out=sg.t[:, 0:tl], in0=bb.t[:, 0:tl], in1=sa.t[:, 0:tl], op=ALU.mult),
                                               waits=[tkb, tsl] + sg.ww())
                                    bb.rd(tmu)
                                    sa.rd(tmu)
                                    sg.written(tmu)
                                    tst = k.dma("sp", ACTT[hc][:, t0:t0 + tl], sg.t[:, 0:tl], ds, waits=[tmu])
                                    sg.rd(tst)
                            pf.done(ta_i, tkb)
                            pf.done(tb_i, tkb)
                        k.barrier()
                stop_if(f'E1_{l}')
                with ExitStack() as ps:
                    alloc_ring(ps)
                    AL = [Buf(sb(f"AL{i}", [128, 44, 512], BF16, ps)) for i in range(2)]
                    alds = [k.dsem(f"alds{i}") for i in range(2)]
                    stg = Stage("f_stg", [128, 512], F32, 3, ps)
                    al_state = {"i": 0}
                    al_pend = [None]

                    def issue_al(t0, tl):
                        i = al_state["i"]
                        al_state["i"] += 1
                        al = AL[i % 2]
                        tal = k.dma("sp", al.t[:, :, 0:tl], ACTT.rearrange("a p t -> p a t")[:, :, t0:t0 + tl], alds[i % 2], waits=al.ww())
                        al.written(tal)
                        return al, tal
                    wfo = w_ffn_out[l]
                    for n in range(4):
                        tis = []
                        kcs = [16, 16, 12]
                        for gi_, h0 in enumerate([0, 16, 32]):
                            nk = kcs[gi_]
                            src = wfo.rearrange("(hc p) m -> p hc m", p=128)[:, h0:h0 + nk, n * 512:(n + 1) * 512]

                            def fn(slot, emit, src=src, nk=nk):
                                emit(slot[:, 0:nk, :], src)
                            tis.append(pf.add(fn))
                        slots = [pf.get(ti) for ti in tis]
                        lastk = None
                        tls = tt_list(chunks_upd)
                        for tix, (t0, tl) in enumerate(tls):
                            if al_pend[0] is None:
                                al_pend[0] = issue_al(t0, tl)
                            al, tal = al_pend[0]
                            if tix + 1 < len(tls):
                                al_pend[0] = issue_al(*tls[tix + 1])
                            elif n + 1 < 4:
                                al_pend[0] = issue_al(*tls[0])
                            else:
                                al_pend[0] = None
                            for ci in range(tl // 128):
                                c = t0 // 128 + ci
                                bk = next_bank()
                                kk = 0
                                for slot, nk in zip(slots, kcs):
                                    for kq in range(nk):
                                        lastk = k.op("pe", lambda e, bk=bk, al=al, kk=kk, ci=ci, slot=slot, kq=kq: e.matmul(
                                            bk.t[:], lhsT=al.t[:, kk, ci * 128:(ci + 1) * 128], rhs=slot.t[:, kq, :], start=(kk == 0), stop=(kk == 43)),
                                            waits=slot.readys + [tal] + bk.ww(), inc=(kk == 43))
                                        kk += 1
                                bk.written(lastk)
                                sg, ds = stg.next()
                                if c % 2 == 0:
                                    te = k.op("act", lambda e, sg=sg, bk=bk: e.activation(out=sg.t[:], in_=bk.t[:], func=AF.Identity), waits=[lastk] + sg.ww())
                                else:
                                    te = k.op("dve", lambda e, sg=sg, bk=bk: e.tensor_copy(out=sg.t[:], in_=bk.t[:]), waits=[lastk] + sg.ww())
                                bk.rd(te)
                                sg.written(te)
                                tst = k.dma("sp", YS[c][:, n * 512:(n + 1) * 512], sg.t[:], ds, waits=[te])
                                sg.rd(tst)
                            al.rd(lastk)
                        for ti in tis:
                            pf.done(ti, lastk)
                    k.barrier()
                if last:
                    epilogue_pass(l, chunks_upd, XS, lambda c: out[(c - 2) * 128:(c - 1) * 128, :], 5, g_post_ffn)

        try:
            build_body()
        except StopBuild:
            pass

        k.barrier()
        print("n_sbuf_tensors", uniq[0])
        print("sem counts:", {e: k.prog[e].cnt for e in k.ENG}, "n_dsems", len(k.dsems),
              "n_ops", {e: len(k.ops[e]) for e in k.ENG})

        block = es.enter_context(nc.Block())

        @block.sync
        def _(e):
            k.replay("sp", e)

        @block.scalar
        def _(e):
            k.replay("act", e)

        @block.vector
        def _(e):
            k.replay("dve", e)

        @block.gpsimd
        def _(e):
            k.replay("pool", e)

        @block.tensor
        def _(e):
            k.replay("pe", e)
    return nc


def make_consts():
    c = np.zeros((128, 1408), np.float32)
    c[:, 0:128] = np.eye(128, dtype=np.float32)
    j = np.arange(128)[:, None]
    i = np.arange(128)[None, :]
    trif = (j <= i).astype(np.float32)
    trir = (j >= i).astype(np.float32)
    c[:, 128:256] = trif
    c[:, 256:384] = trir
    c[:, 384:896] = np.tile(trif, (1, 4))
    c[:, 896:1408] = np.tile(trir, (1, 4))
    return c


def kernel(x, c, ctx, c_ctx, w_mod, b_mod, g_pre_mix, g_post_mix, g_pre_ffn, g_post_ffn, w_in, gmlp_ln_g, gmlp_ln_b,
           gmlp_ws, gmlp_bs, gla_wd2_fwd, gla_bd_fwd, gla_wd2_bwd, gla_bd_bwd, gla_out_g, w_out, w_ffn_in, w_ffn_out):
    f = lambda a: np.ascontiguousarray(np.asarray(a, dtype=np.float32))
    x, c, ctx, c_ctx = f(x), f(c), f(ctx), f(c_ctx)
    shared = {
        "w_mod": f(w_mod), "b_mod": f(b_mod), "g_pre_mix": f(g_pre_mix), "g_post_mix": f(g_post_mix),
        "g_pre_ffn": f(g_pre_ffn), "g_post_ffn": f(g_post_ffn), "w_in": f(w_in), "ln_g": f(gmlp_ln_g), "ln_b": f(gmlp_ln_b),
        "out_g": f(gla_out_g), "w_out": f(w_out), "w_ffn_in": f(w_ffn_in), "w_ffn_out": f(w_ffn_out), "consts": make_consts(),
    }
    w_in_np = shared["w_in"]
    ws = f(gmlp_ws)
    bs = f(gmlp_bs)
    wf, bf_, wb, bb_ = f(gla_wd2_fwd), f(gla_bd_fwd), f(gla_wd2_bwd), f(gla_bd_bwd)
    role = []
    for half in range(2):
        if half == 0:
            w_d = np.ascontiguousarray(w_in_np[:, :, 5120:5152])
            wsT = np.ascontiguousarray(ws.transpose(0, 3, 1, 2))
            bsr = np.ascontiguousarray(bs.reshape(L, 1024))
            wF, bF, wR, bR = wf, bf_, wb, bb_
        else:
            w_d = np.ascontiguousarray(np.concatenate([w_in_np[:, :, 5136:5152], w_in_np[:, :, 5120:5136]], axis=2))
            wsf = ws[:, :, ::-1, ::-1]
            wsT = np.ascontiguousarray(wsf.transpose(0, 3, 1, 2))
            bsr = np.ascontiguousarray(bs[:, :, ::-1].reshape(L, 1024))
            wF, bF, wR, bR = wb, bb_, wf, bf_
        wd = np.zeros((L, 33, 1024), np.float32)
        wd[:, 0:16, 0:512] = wF
        wd[:, 16:32, 512:1024] = wR
        wd[:, 32, 0:512] = bF
        wd[:, 32, 512:1024] = bR
        sel = np.zeros((128, 2), np.float32)
        sel[:, 1 - half] = 1.0
        role.append({"w_d": w_d, "wsT": wsT, "bsr": bsr, "wdaug": wd, "sel": sel})
    in_maps = []
    for r in range(8):
        b, half = r // 2, r % 2
        if half == 0:
            xi = np.concatenate([ctx[b], x[b, 0:2048]], axis=0)
        else:
            xi = np.concatenate([ctx[b, ::-1], x[b, 4095:2047:-1]], axis=0)
        m = dict(shared)
        m.update(role[half])
        m["xin"] = np.ascontiguousarray(xi)
        m["cvec"] = np.ascontiguousarray(np.stack([c[b], c_ctx], axis=0))
        in_maps.append(m)
    nc = build_program()
    res = run_bass_kernel_spmd(nc, in_maps, core_ids=list(range(8)))
    outp = np.empty((4, 4096, 2048), np.float32)
    for r in range(8):
        b, half = r // 2, r % 2
        o = np.asarray(res.results[r]["out"], dtype=np.float32)
        if half == 0:
            outp[b, 0:2048] = o
        else:
            outp[b, 2048:4096] = o[::-1]
    return outp
```
